# Optimizing a Trainium2 kernel written in Bass

```python
import jax, jax.numpy as jnp
from jax import lax
import numpy as np

D_MODEL = 2048
BATCH = 1
SEQ = 16384
DEPTH = 1
DEC_BATCH = 2
DEC_SEQ = 16384
PAST_LEN = 128

HEAD_DIM = 128
N_HEADS = 8
N_KV_HEADS = 2
ATTN_DIM = N_HEADS * HEAD_DIM
KV_DIM = N_KV_HEADS * HEAD_DIM
CONV_DIM = D_MODEL - ATTN_DIM
CONV_WIDTH = 31
CONV_PAD = CONV_WIDTH // 2
IN_DIM = ATTN_DIM + 2 * KV_DIM + 2 * CONV_DIM
D_FF = ((8 * D_MODEL + 3 * 256 - 1) // (3 * 256)) * 256
GRID_W = 64
ROPE_THETA = 10000.0
ROPE_AXIS_DIM = HEAD_DIM // 2
Q_BLOCK = 128
EPS = 1e-6

kernel_name = "hybrid_gqa_conformer_conv_encoder"


def rms_norm(x, g):
    xf = x.astype(jnp.float32)
    y = xf * lax.rsqrt(jnp.mean(xf * xf, axis=-1, keepdims=True) + EPS)
    return (y * g.astype(jnp.float32)).astype(x.dtype)


def axial_rope_tables(S):
    rows = S // GRID_W
    row = jnp.repeat(jnp.arange(rows, dtype=jnp.float32), GRID_W)
    col = jnp.tile(jnp.arange(GRID_W, dtype=jnp.float32), rows)
    n_freq = ROPE_AXIS_DIM // 2
    inv = ROPE_THETA ** (-(jnp.arange(n_freq, dtype=jnp.float32) * 2.0 / ROPE_AXIS_DIM))
    ang_r = row[:, None] * inv[None, :]
    ang_c = col[:, None] * inv[None, :]
    return jnp.cos(ang_r), jnp.sin(ang_r), jnp.cos(ang_c), jnp.sin(ang_c)


def _rot(v, cos, sin):
    half = v.shape[-1] // 2
    v1, v2 = v[..., :half], v[..., half:]
    c = cos[None, :, None, :]
    s = sin[None, :, None, :]
    return jnp.concatenate([v1 * c - v2 * s, v2 * c + v1 * s], axis=-1)


def apply_axial_rope(x, tabs):
    cos_r, sin_r, cos_c, sin_c = tabs
    xf = x.astype(jnp.float32)
    out = jnp.concatenate([_rot(xf[..., :ROPE_AXIS_DIM], cos_r, sin_r),
                           _rot(xf[..., ROPE_AXIS_DIM:], cos_c, sin_c)], axis=-1)
    return out.astype(x.dtype)


def blocked_gqa_attention(q, k, v):
    B, S, H, D = q.shape
    G = H // N_KV_HEADS
    nb = S // Q_BLOCK
    scale = 1.0 / np.sqrt(D)
    qb = q.reshape(B, nb, Q_BLOCK, N_KV_HEADS, G, D).transpose(1, 0, 2, 3, 4, 5)

    def one_block(qblk):
        s = jnp.einsum('bqkgd,bskd->bkgqs', qblk, k).astype(jnp.float32) * scale
        p = jax.nn.softmax(s, axis=-1).astype(v.dtype)
        return jnp.einsum('bkgqs,bskd->bqkgd', p, v)

    o = lax.map(one_block, qb)
    return o.transpose(1, 0, 2, 3, 4, 5).reshape(B, S, H * D)


def conformer_conv(u, w_dw, b_dw, g_cn, b_cn):
    a, gt = jnp.split(u, 2, axis=-1)
    h = a * jax.nn.sigmoid(gt)
    h = lax.conv_general_dilated(h, w_dw[:, None, :].astype(h.dtype), window_strides=(1,),
                                 padding=[(CONV_PAD, CONV_PAD)],
                                 dimension_numbers=('NWC', 'WIO', 'NWC'),
                                 feature_group_count=CONV_DIM) + b_dw
    hf = h.astype(jnp.float32)
    mu = jnp.mean(hf, axis=-1, keepdims=True)
    var = jnp.mean(jnp.square(hf - mu), axis=-1, keepdims=True)
    y = (hf - mu) * lax.rsqrt(var + EPS) * g_cn.astype(jnp.float32) + b_cn.astype(jnp.float32)
    return (y * jax.nn.sigmoid(y)).astype(u.dtype)


def encoder_layer(x, c, tabs, w_ada, b_ada, g_norm1, w_in, b_in, g_q, g_k, w_dw, b_dw,
                  g_cn, b_cn, w_out, g_norm2, w_gate, w_up, w_down):
    B, S, _ = x.shape
    mod = jax.nn.silu(c) @ w_ada + b_ada
    sh1, sc1, gt1, sh2, sc2, gt2 = [m[:, None, :] for m in jnp.split(mod, 6, axis=-1)]

    h = rms_norm(x, g_norm1) * (1.0 + sc1) + sh1
    z = h @ w_in + b_in
    q = z[..., :ATTN_DIM].reshape(B, S, N_HEADS, HEAD_DIM)
    k = z[..., ATTN_DIM:ATTN_DIM + KV_DIM].reshape(B, S, N_KV_HEADS, HEAD_DIM)
    v = z[..., ATTN_DIM + KV_DIM:ATTN_DIM + 2 * KV_DIM].reshape(B, S, N_KV_HEADS, HEAD_DIM)
    u = z[..., ATTN_DIM + 2 * KV_DIM:]
    q = apply_axial_rope(rms_norm(q, g_q), tabs)
    k = apply_axial_rope(rms_norm(k, g_k), tabs)
    attn = blocked_gqa_attention(q, k, v)
    conv = conformer_conv(u, w_dw, b_dw, g_cn, b_cn)
    mix = jnp.concatenate([attn, conv], axis=-1) @ w_out
    x = x + gt1 * mix

    h = rms_norm(x, g_norm2) * (1.0 + sc2) + sh2
    f = (jax.nn.silu(h @ w_gate) * (h @ w_up)) @ w_down
    return x + gt2 * f


def setup_inputs(seed: int = 0) -> dict:
    key = jax.random.key(seed)
    ks = jax.random.split(key, 24)
    f32 = jnp.float32
    nrm = lambda k, shp, s: jax.random.normal(k, shp, f32) * s
    return {
        "x_prompt": nrm(ks[0], (BATCH, SEQ, D_MODEL), 1.0),
        "x_sample": nrm(ks[1], (DEC_BATCH, DEC_SEQ, D_MODEL), 1.0),
        "c_prompt": nrm(ks[2], (BATCH, D_MODEL), 1.0),
        "c_sample": nrm(ks[3], (DEC_BATCH, D_MODEL), 1.0),
        "w_ada": nrm(ks[4], (DEPTH, D_MODEL, 6 * D_MODEL), 0.2 * D_MODEL ** -0.5),
        "b_ada": nrm(ks[5], (DEPTH, 6 * D_MODEL), 0.02),
        "g_norm1": 1.0 + nrm(ks[6], (DEPTH, D_MODEL), 0.02),
        "w_in": nrm(ks[7], (DEPTH, D_MODEL, IN_DIM), D_MODEL ** -0.5),
        "b_in": nrm(ks[8], (DEPTH, IN_DIM), 0.02),
        "g_q": 1.0 + nrm(ks[9], (DEPTH, HEAD_DIM), 0.02),
        "g_k": 1.0 + nrm(ks[10], (DEPTH, HEAD_DIM), 0.02),
        "w_dw": nrm(ks[11], (DEPTH, CONV_WIDTH, CONV_DIM), CONV_WIDTH ** -0.5),
        "b_dw": nrm(ks[12], (DEPTH, CONV_DIM), 0.02),
        "g_cn": 1.0 + nrm(ks[13], (DEPTH, CONV_DIM), 0.02),
        "b_cn": nrm(ks[14], (DEPTH, CONV_DIM), 0.02),
        "w_out": nrm(ks[15], (DEPTH, D_MODEL, D_MODEL), D_MODEL ** -0.5),
        "g_norm2": 1.0 + nrm(ks[16], (DEPTH, D_MODEL), 0.02),
        "w_gate": nrm(ks[17], (DEPTH, D_MODEL, D_FF), D_MODEL ** -0.5),
        "w_up": nrm(ks[18], (DEPTH, D_MODEL, D_FF), D_MODEL ** -0.5),
        "w_down": nrm(ks[19], (DEPTH, D_FF, D_MODEL), D_FF ** -0.5),
        "g_final": 1.0 + nrm(ks[20], (D_MODEL,), 0.02),
    }


def reference(x_prompt, x_sample, c_prompt, c_sample, w_ada, b_ada, g_norm1, w_in, b_in,
              g_q, g_k, w_dw, b_dw, g_cn, b_cn, w_out, g_norm2, w_gate, w_up, w_down, g_final):
    def run(x, c):
        tabs = axial_rope_tables(x.shape[1])
        for l in range(DEPTH):
            x = encoder_layer(x, c, tabs, w_ada[l], b_ada[l], g_norm1[l], w_in[l], b_in[l],
                              g_q[l], g_k[l], w_dw[l], b_dw[l], g_cn[l], b_cn[l], w_out[l],
                              g_norm2[l], w_gate[l], w_up[l], w_down[l])
        return rms_norm(x, g_final)

    y_prompt = run(x_prompt, c_prompt)
    y_sample = run(x_sample, c_sample)
    return (y_prompt, y_sample)
```

```python
import os
import numpy as np
import ml_dtypes
import concourse.bass as bass
import concourse.mybir as mybir
from concourse.bass_utils import run_bass_kernel_spmd

F32 = mybir.dt.float32
BF16 = mybir.dt.bfloat16
AF = mybir.ActivationFunctionType
ALU = mybir.AluOpType

NCORES = 8
D = 2048
SEQ = 16384
NSEQ = 3
TL = SEQ // NCORES
T = 512
NT = TL // T
KC = D // 128
DFF = 5632
FC = DFF // 128
IN_DIM = 3584
EPS = 1e-6
SCALE = 1.0 / np.sqrt(128.0)

P_G1, P_G2, P_GF, P_BIN, P_GQ, P_GK, P_BDW, P_GCN, P_BCN = 0, 16, 32, 48, 76, 77, 78, 86, 94
P_ROWS = 102


class Tracker:
    def __init__(self):
        self.ops = []
        self.last_w = {}
        self.readers = {}

    def op(self, eng, fn, r=(), w=(), dma=None, inc=16):
        i = len(self.ops)
        deps = set()
        for k in r:
            if k in self.last_w:
                deps.add(self.last_w[k])
        for k in w:
            if k in self.last_w:
                deps.add(self.last_w[k])
            deps.update(self.readers.get(k, {}).values())
        for k in r:
            self.readers.setdefault(k, {})[(eng, dma)] = i
        for k in w:
            self.last_w[k] = i
            self.readers[k] = {}
        self.ops.append(dict(eng=eng, fn=fn, deps=deps, dma=dma, inc=inc))
        return i

    def emit(self, nc):
        ops = self.ops
        engs = ["pe", "act", "dve", "pool", "sp"]
        need = set()
        for o in ops:
            for d in o["deps"]:
                P = ops[d]
                if P["dma"] is None and not (P["eng"] == "pe" and o["eng"] == "pe"):
                    need.add(d)
        cnt = {e: 0 for e in engs}
        for i, o in enumerate(ops):
            if o["dma"] is None and i in need:
                cnt[o["eng"]] += 1
                o["sig"] = cnt[o["eng"]]
        dma_names = []
        for o in ops:
            if o["dma"] is not None and o["dma"] not in dma_names:
                dma_names.append(o["dma"])
        sems = {e: nc.alloc_semaphore("e_" + e) for e in engs if e != "sp"}
        dsem = {n: nc.alloc_semaphore("d_" + n) for n in dma_names}
        run = {n: 0 for n in dma_names}
        for i, o in enumerate(ops):
            waits = {}
            for d in o["deps"]:
                P = ops[d]
                if P["dma"] is not None:
                    key = ("d", P["dma"])
                    waits[key] = max(waits.get(key, 0), run[P["dma"]])
                else:
                    if P["eng"] == "pe" and o["eng"] == "pe":
                        continue
                    key = ("e", P["eng"])
                    waits[key] = max(waits.get(key, 0), P["sig"])
            o["waits"] = waits
            if o["dma"] is not None:
                run[o["dma"]] += o["inc"]
        final = dict(run)
        streams = {e: [o for o in ops if o["eng"] == e] for e in engs}

        def run_stream(e, eng):
            known = {}
            for o in streams[e]:
                for key, val in o["waits"].items():
                    if known.get(key, 0) >= val:
                        continue
                    known[key] = val
                    s = dsem[key[1]] if key[0] == "d" else sems[key[1]]
                    eng.wait_ge(s, val)
                inst = o["fn"](eng)
                if o["dma"] is not None:
                    inst.then_inc(dsem[o["dma"]], o["inc"])
                elif "sig" in o:
                    inst.then_inc(sems[e], 1)
            if e == "sp":
                for n, v in final.items():
                    if v > 0:
                        eng.wait_ge(dsem[n], v)

        with nc.Block() as block:
            @block.sync
            def _(eng):
                run_stream("sp", eng)

            @block.tensor
            def _(eng):
                run_stream("pe", eng)

            @block.scalar
            def _(eng):
                run_stream("act", eng)

            @block.vector
            def _(eng):
                run_stream("dve", eng)

            @block.gpsimd
            def _(eng):
                run_stream("pool", eng)


def build(stage=2):
    nc = bass.Bass("TRN2", target_bir_lowering=False)
    TR = Tracker()

    def din(name, shape, dt=F32):
        return nc.dram_tensor(name, list(shape), dt, kind="ExternalInput")

    xs = din("xs", [NSEQ, TL, D])
    xh = din("xh", [NSEQ, NT, 32, D])
    c48 = din("c48", [48, 128])
    w_ada = din("w_ada", [D, 1536])
    b_ada = din("b_ada", [1, 1536])
    prm = din("prm", [P_ROWS, 128])
    bvrow_d = din("bvrow", [1, 256])
    w_dw = din("w_dw", [31, 1024])
    w_in = din("w_in", [256, IN_DIM])
    w_out = din("w_out", [256, D])
    w_gate = din("w_gate", [256, DFF])
    w_up = din("w_up", [256, DFF])
    w_down = din("w_down", [704, D])
    ident_d = din("ident", [128, 128])
    rotT_d = din("rotT", [128, 128])
    cos_d = din("cos_t", [128, TL])
    sin_d = din("sin_t", [128, TL])
    hmask_d = din("hmask", [128, 2])
    y = nc.dram_tensor("y", [NSEQ, TL, D], F32, kind="ExternalOutput")

    wshapes = dict(win=(256, IN_DIM), wout=(256, D), wgate=(256, DFF), wup=(256, DFF), wdown=(704, D))
    wsrc = dict(win=w_in, wout=w_out, wgate=w_gate, wup=w_up, wdown=w_down)
    wb_s = {k: nc.dram_tensor("wbs_" + k, [r, c], BF16) for k, (r, c) in wshapes.items()}
    wb_g = {k: nc.dram_tensor("wbg_" + k, [r * NCORES, c], BF16) for k, (r, c) in wshapes.items()}
    mod_s = nc.dram_tensor("mod_s", [3, 1536], F32)
    mod_g = nc.dram_tensor("mod_g", [3 * NCORES, 1536], F32)
    kT_s = nc.dram_tensor("kT_s", [NSEQ * 256, TL], BF16)
    kT_g = nc.dram_tensor("kT_g", [NSEQ * 256 * NCORES, TL], BF16)
    v_s = nc.dram_tensor("v_s", [NSEQ * 256, TL], BF16)
    v_g = nc.dram_tensor("v_g", [NSEQ * 256 * NCORES, TL], BF16)

    dbg = {}
    if stage <= 1:
        dbg["k"] = nc.dram_tensor("dbg_k", [NSEQ * 256, TL], BF16, kind="ExternalOutput")
        dbg["v"] = nc.dram_tensor("dbg_v", [NSEQ * 256, TL], BF16, kind="ExternalOutput")
        dbg["mod"] = nc.dram_tensor("dbg_mod", [128, 288], F32, kind="ExternalOutput")

    sb = lambda name, shape, dt: nc.alloc_sbuf_tensor("sb_" + name, shape, dt)
    ident = sb("ident", [128, 128], F32)
    rotT = sb("rotT", [128, 128], F32)
    ones_f = sb("ones_f", [128, 128], F32)
    ones_b = sb("ones_b", [128, 128], BF16)
    epsT = sb("epsT", [128, 1], F32)
    prmP = sb("prmP", [128, P_ROWS], F32)
    wdwP = sb("wdwP", [128, 8, 31], F32)
    modp = sb("modp", [128, 3, 96], F32)
    A1 = sb("A1", [128, 3, 16], F32)
    A2 = sb("A2", [128, 3, 16], F32)
    hmask = sb("hmask", [128, 2], F32)
    bvrow = sb("bvrow", [1, 256], F32)
    bada = sb("bada", [1, 1536], F32)
    scT = sb("scT", [128, 48], F32)
    mods = sb("mods", [3, 1536], F32)
    cs = sb("cs", [128, 2, T], F32)
    xT = sb("xT", [128, KC, T], F32)
    xTh = sb("xTh", [128, KC, 32], F32)
    stg = sb("stg", [128, 2, D], F32)
    hb = sb("hb", [128, KC, T], BF16)
    hbh = sb("hbh", [128, KC, 32], BF16)
    ao = sb("ao", [128, 8, T], BF16)
    glu = sb("glu", [128, 2, 544], F32)
    gluh = sb("gluh", [128, 64], F32)
    dummy = sb("dummy", [128, 2], F32)
    tmpf = sb("tmpf", [128, 4, T], F32)
    sqb = sb("sqb", [128, 2, T], BF16)
    PT = sb("PT", [128, 3, T], BF16)
    wring = sb("wring", [128, 3, 5632], BF16)
    arena = sb("arena", [128, 22528], BF16)
    actT = arena[:, :].rearrange("p (a b) -> p a b", b=T)
    qT = arena[:, 0:4096].rearrange("p (a b) -> p a b", b=T)
    cacc = arena[:, 4096:12288].bitcast(F32).rearrange("p (a b) -> p a b", b=T)
    Kring = arena[:, 12288:16384].rearrange("p (a b) -> p a b", b=2048)
    Vring = arena[:, 16384:20480].rearrange("p (a b) -> p a b", b=2048)
    wkv = arena[:, 0:8192].rearrange("p (a b) -> p a b", b=512)
    kst = arena[:, 8192:9216].rearrange("p (a b) -> p a b", b=T)
    vst = arena[:, 9216:10240].rearrange("p (a b) -> p a b", b=256)
    wa = arena[:, 0:16384].bitcast(F32).rearrange("p (a b) -> p a b", b=512)
    ps = [nc.alloc_psum_tensor("ps%d" % i, [128, 512], F32) for i in range(8)]

    op = TR.op
    state = dict(stg=0, tmp=0, w=0, sq=0, psb=0, ev=0)

    def nxt(name, n):
        v = state[name]
        state[name] = (v + 1) % n
        return v

    def dma(eng, out, in_, r, w, sem, **kw):
        return op(eng, lambda e: e.dma_start(out=out, in_=in_, **kw), r=r, w=w, dma=sem)

    def mm(out, lhsT, rhs, start, stop, r, w):
        return op("pe", lambda e: e.matmul(out, lhsT, rhs, start=start, stop=stop), r=r, w=w)

    def tp(out, in_, idn, r, w):
        return op("pe", lambda e: e.transpose(out, in_, idn), r=r, w=w)

    def act(out, in_, func, r, w, bias=0.0, scale=1.0):
        return op("act", lambda e: e.activation(out, in_, func, bias=bias, scale=scale), r=r, w=w)

    def ts(eng, out, in0, s1, s2, op0, op1, r, w):
        if s2 is None:
            return op(eng, lambda e: e.tensor_scalar(out, in0, s1, None, op0), r=r, w=w)
        return op(eng, lambda e: e.tensor_scalar(out, in0, s1, s2, op0, op1), r=r, w=w)

    def stt(out, in0, sc, in1, op0, op1, r, w):
        return op("dve", lambda e: e.scalar_tensor_tensor(out, in0, sc, in1, op0, op1), r=r, w=w)

    def tt(eng, out, in0, in1, o, r, w):
        return op(eng, lambda e: e.tensor_tensor(out, in0, in1, o), r=r, w=w)

    def cp(eng, out, in_, r, w):
        if eng == "act":
            return op("act", lambda e: e.activation(out, in_, AF.Copy), r=r, w=w)
        return op(eng, lambda e: e.tensor_copy(out, in_), r=r, w=w)

    def recip(out, in_, r, w):
        return op("dve", lambda e: e.reciprocal(out, in_), r=r, w=w)

    def evac_eng():
        return "dve" if nxt("ev", 2) == 0 else "act"

    dma("sp", ident[:, :], ident_d[:, :], [], ["const"], "setup")
    dma("sp", rotT[:, :], rotT_d[:, :], [], ["const"], "setup")
    dma("sp", hmask[:, :], hmask_d[:, :], [], ["const"], "setup")
    dma("sp", bvrow[:, :], bvrow_d[:, :], [], ["const"], "setup")
    dma("sp", bada[:, :], b_ada[:, :], [], ["const"], "setup")
    op("dve", lambda e: e.memset(ones_f[:, :], 1.0), w=["ones_f"])
    op("dve", lambda e: e.memset(ones_b[:, :], 1.0), w=["ones_b"])
    op("dve", lambda e: e.memset(epsT[:, :], EPS), w=["eps"])
    nonce = float(int.from_bytes(os.urandom(2), "little"))
    op("dve", lambda e: e.memset(dummy[:, 1:2], nonce), w=["nonce"])

    def cast_and_gather(k):
        r, c = wshapes[k]
        src = wsrc[k][:, :].rearrange("r (a b) -> r a b", b=512)
        dst = wb_s[k][:, :].rearrange("r (a b) -> r a b", b=512)
        dma("pool", dst, src, [], ["wbs_" + k], "cast_" + k)
        op("pool", lambda e: e.collective_compute(
            "AllGather", ALU.bypass, replica_groups=[list(range(NCORES))],
            ins=[wb_s[k].ap().opt()], outs=[wb_g[k].ap().opt()]),
           r=["wbs_" + k], w=["wbg_" + k, "cc_chain"], dma="ag_" + k, inc=1)

    cast_and_gather("win")

    dma("sp", stg[0:P_ROWS, 0, 0:128], prm[:, :], [], ["stg0"], "stg0")
    tp(ps[0][:, 0:P_ROWS], stg[0:P_ROWS, 0, 0:128], ident[0:P_ROWS, 0:P_ROWS], ["stg0", "const"], ["ps0"])
    cp("dve", prmP[:, :], ps[0][:, 0:P_ROWS], ["ps0"], ["prmP"])
    dma("sp", stg[0:31, 1, 0:1024], w_dw[:, :], [], ["stg1"], "stg1")
    for cc in range(8):
        tp(ps[1][:, cc * 32:cc * 32 + 31], stg[0:31, 1, cc * 128:(cc + 1) * 128], ident[0:31, 0:31],
           ["stg1", "const"], ["ps1"])
    cp("dve", wdwP[:, :, :], ps[1][:, 0:256].rearrange("p (a b) -> p a b", b=32)[:, :, 0:31], ["ps1"], ["wdwP"])
    dma("sp", stg[0:48, 0, 0:128], c48[:, :], ["stg0"], ["stg0"], "stg0")
    tp(ps[2][:, 0:48], stg[0:48, 0, 0:128], ident[0:48, 0:48], ["stg0", "const"], ["ps2"])
    act(scT[:, :], ps[2][:, 0:48], AF.Silu, ["ps2"], ["scT"])
    for cg in range(3):
        dma("sp", wa, w_ada[:, cg * 512:(cg + 1) * 512].rearrange("(kc p) c -> p kc c", p=128),
            [], ["arenaA"], "wa")
        b = 3
        for kc in range(KC):
            mm(ps[b][0:3, :], scT[:, kc * 3:(kc + 1) * 3], wa[:, kc, :], kc == 0, False,
               ["scT", "arenaA"], ["ps3"])
        mm(ps[b][0:3, :], ones_f[0:1, 0:3], bada[0:1, cg * 512:(cg + 1) * 512], False, True,
           ["ones_f", "const"], ["ps3"])
        cp("dve", mods[:, cg * 512:(cg + 1) * 512], ps[b][0:3, :], ["ps3"], ["mods"])
    dma("sp", mod_s[:, :], mods[:, :], ["mods"], ["mod_s"], "mods")
    op("pool", lambda e: e.collective_compute(
        "AllGather", ALU.bypass, replica_groups=[list(range(NCORES))],
        ins=[mod_s.ap().opt()], outs=[mod_g.ap().opt()]),
       r=["mod_s"], w=["mod_g", "cc_chain"], dma="ag_mod", inc=1)
    for b in range(3):
        slot = b % 2
        for r_ in range(NCORES):
            dma("sp", stg[r_ * 12:(r_ + 1) * 12, slot, 0:128],
                mod_g[r_ * 3 + b, :].rearrange("(cj e) -> cj e", e=128),
                ["mod_g"], ["stg%d" % slot], "stg%d" % slot)
        pb = 4 + b
        tp(ps[pb][:, 0:96], stg[0:96, slot, 0:128], ident[0:96, 0:96], ["stg%d" % slot, "const"], ["ps%d" % pb])
        cp("dve", modp[:, b, :], ps[pb][:, 0:96], ["ps%d" % pb], ["modp"])
        stt(A1[:, b, :], modp[:, b, 16:32], 1.0, prmP[:, P_G1:P_G1 + 16], ALU.add, ALU.mult,
            ["modp", "prmP"], ["A1"])
        stt(A2[:, b, :], modp[:, b, 64:80], 1.0, prmP[:, P_G2:P_G2 + 16], ALU.add, ALU.mult,
            ["modp", "prmP"], ["A2"])
    if stage <= 1:
        dma("sp", dbg["mod"][:, :], modp[:, :, :].rearrange("p a b -> p (a b)"), ["modp"], ["dbgm"], "dbg")
    if stage == 0:
        TR.emit(nc)
        return nc

    for k in ("wout", "wgate", "wup", "wdown"):
        cast_and_gather(k)

    def load_x_tile(s, ti):
        t0 = ti * T
        for sub in range(4):
            slot = nxt("stg", 2)
            sk = "stg%d" % slot
            dma("sp", stg[:, slot, :], xs[s, t0 + sub * 128:t0 + (sub + 1) * 128, :], [], [sk], sk)
            for kg in range(4):
                b = nxt("psb", 4)
                for j in range(4):
                    kc = kg * 4 + j
                    tp(ps[b][:, j * 128:(j + 1) * 128], stg[:, slot, kc * 128:(kc + 1) * 128], ident[:, :],
                       [sk, "const"], ["ps%d" % b])
                cp(evac_eng(), xT[:, kg * 4:kg * 4 + 4, sub * 128:(sub + 1) * 128],
                   ps[b][:, :].rearrange("p (a b) -> p a b", b=128), ["ps%d" % b],
                   ["xT%d" % (kg * 4 + j) for j in range(4)])

    def load_halo(s, ti):
        slot = nxt("stg", 2)
        sk = "stg%d" % slot
        dma("sp", stg[0:32, slot, :], xh[s, ti, :, :], [], [sk], sk)
        for kg in range(4):
            b = nxt("psb", 4)
            for j in range(4):
                kc = kg * 4 + j
                tp(ps[b][:, j * 32:(j + 1) * 32], stg[0:32, slot, kc * 128:(kc + 1) * 128], ident[0:32, 0:32],
                   [sk, "const"], ["ps%d" % b])
            cp(evac_eng(), xTh[:, kg * 4:kg * 4 + 4, :],
               ps[b][:, 0:128].rearrange("p (a b) -> p a b", b=32), ["ps%d" % b], ["xTh"])

    def norm(src, skeys, dst, dkeys, A, B, N, nfeat):
        b = nxt("psb", 4)
        pk = "ps%d" % b
        for kc in range(KC):
            q = nxt("sq", 2)
            act(sqb[:, q, 0:N], src[:, kc, 0:N], AF.Square, [skeys[kc]], ["sq%d" % q])
            mm(ps[b][:, 0:N], ones_b[:, :], sqb[:, q, 0:N], kc == 0, kc == KC - 1, ["sq%d" % q, "ones_b"], [pk])
        ta = nxt("tmp", 4)
        act(tmpf[:, ta, 0:N], ps[b][:, 0:N], AF.Sqrt, [pk, "eps"], ["tmp%d" % ta], bias=epsT[:, 0:1], scale=1.0 / nfeat)
        recip(tmpf[:, ta, 0:N], tmpf[:, ta, 0:N], ["tmp%d" % ta], ["tmp%d" % ta])
        for kc in range(KC):
            if B is None:
                stt(dst[:, kc, 0:N], src[:, kc, 0:N], A[:, kc:kc + 1], tmpf[:, ta, 0:N], ALU.mult, ALU.mult,
                    [skeys[kc], "tmp%d" % ta, "A"], [dkeys[kc]])
            else:
                tb = nxt("tmp", 4)
                if tb == ta:
                    tb = nxt("tmp", 4)
                stt(tmpf[:, tb, 0:N], src[:, kc, 0:N], A[:, kc:kc + 1], tmpf[:, ta, 0:N], ALU.mult, ALU.mult,
                    [skeys[kc], "tmp%d" % ta, "A1", "A2"], ["tmp%d" % tb])
                ts("pool", dst[:, kc, 0:N], tmpf[:, tb, 0:N], B[:, kc:kc + 1], 1.0, ALU.add, ALU.mult,
                   ["tmp%d" % tb, "modp"], [dkeys[kc]])

    xTk = ["xT%d" % i for i in range(KC)]
    hbk = ["hb%d" % i for i in range(KC)]

    def qk_post(pq, pqk, bias_ap, g_ap, dst, dkey):
        ta = nxt("tmp", 4)
        tak = "tmp%d" % ta
        act(tmpf[:, ta, :], pq, AF.Identity, [pqk, "prmP"], [tak], bias=bias_ap)
        q = nxt("sq", 2)
        act(sqb[:, q, :], tmpf[:, ta, :], AF.Square, [tak], ["sq%d" % q])
        b = nxt("psb", 4)
        pk = "ps%d" % b
        mm(ps[b][:, :], ones_b[:, :], sqb[:, q, :], True, True, ["sq%d" % q, "ones_b"], [pk])
        tb = nxt("tmp", 4)
        tbk = "tmp%d" % tb
        act(tmpf[:, tb, :], ps[b][:, :], AF.Sqrt, [pk, "eps"], [tbk], bias=epsT[:, 0:1], scale=1.0 / 128.0)
        recip(tmpf[:, tb, :], tmpf[:, tb, :], [tbk], [tbk])
        stt(tmpf[:, ta, :], tmpf[:, ta, :], g_ap, tmpf[:, tb, :], ALU.mult, ALU.mult, [tak, tbk, "prmP"], [tak])
        b2 = nxt("psb", 4)
        pk2 = "ps%d" % b2
        mm(ps[b2][:, :], rotT[:, :], tmpf[:, ta, :], True, True, [tak, "const"], [pk2])
        tt("dve", tmpf[:, tb, :], ps[b2][:, :], cs[:, 1, :], ALU.mult, [pk2, "cs"], [tbk])
        tt("pool", tmpf[:, ta, :], tmpf[:, ta, :], cs[:, 0, :], ALU.mult, [tak, "cs"], [tak])
        tt("pool", dst, tmpf[:, ta, :], tmpf[:, tb, :], ALU.add, [tak, tbk, "arenaA"], [dkey])

    def load_cs(ti):
        t0 = ti * T
        dma("sp", cs[:, 0, :], cos_d[:, t0:t0 + T], [], ["cs"], "cs")
        dma("sp", cs[:, 1, :], sin_d[:, t0:t0 + T], [], ["cs"], "cs")

    def wblock(k, rows_view, cols, nk, ncols):
        slot = nxt("w", 3)
        wk = "w%d" % slot
        view = wring[:, slot, 0:nk * ncols].rearrange("p (a b) -> p a b", b=ncols)
        dma("sp", view, rows_view[:, :, cols[0]:cols[1]], ["wbg_" + k], [wk], wk)
        return view, wk

    def gate(keys):
        op("dve", lambda e: e.memset(dummy[:, 0:1], 0.0), w=list(keys) + ["dummy"])

    win_v = wb_g["win"][:, :].rearrange("(kc p) c -> p kc c", p=128)
    gate(["arenaA", "arenaB"])
    dma("sp", wkv, win_v[:, :, 1024:1536], ["wbg_win"], ["arenaA"], "wkv")
    for s in range(NSEQ):
        for ti in range(NT):
            load_cs(ti)
            load_x_tile(s, ti)
            norm(xT, xTk, hb, hbk, A1[:, s, :], modp[:, s, 0:16], T, float(D))
            for kvh in range(2):
                b = nxt("psb", 4)
                pk = "ps%d" % b
                for kc in range(KC):
                    mm(ps[b][:, :], wkv[:, kc, kvh * 128:(kvh + 1) * 128], hb[:, kc, :], kc == 0, kc == KC - 1,
                       ["arenaA", hbk[kc]], [pk])
                qk_post(ps[b][:, :], pk, prmP[:, P_BIN + 8 + kvh:P_BIN + 9 + kvh], prmP[:, P_GK:P_GK + 1],
                        kst[:, kvh, :], "kst")
            for kvh in range(2):
                dma("sp", kT_s[(s * 2 + kvh) * 128:(s * 2 + kvh + 1) * 128, ti * T:(ti + 1) * T], kst[:, kvh, :],
                    ["kst", "arenaA"], ["kT_s"], "kvst")
            for sub in range(4):
                b = nxt("psb", 4)
                pk = "ps%d" % b
                for kc in range(KC):
                    mm(ps[b][:, 0:256], hb[:, kc, sub * 128:(sub + 1) * 128], wkv[:, kc, 256:512], kc == 0, False,
                       ["arenaA", hbk[kc]], [pk])
                mm(ps[b][:, 0:256], ones_f[0:1, :], bvrow[0:1, :], False, True, ["ones_f", "const"], [pk])
                cp(evac_eng(), vst[:, sub, :], ps[b][:, 0:256], [pk, "arenaA"], ["vst"])
            for kvh in range(2):
                dst = v_s[(s * 2 + kvh) * 128:(s * 2 + kvh + 1) * 128, :].rearrange("p (tb d) -> p tb d", d=128)
                dma("sp", dst[:, ti * 4:(ti + 1) * 4, :], vst[:, :, kvh * 128:(kvh + 1) * 128],
                    ["vst", "arenaA"], ["v_s"], "kvst")
    for nm, src_, dst_ in (("k", kT_s, kT_g), ("v", v_s, v_g)):
        def mk(src_=src_, dst_=dst_):
            return lambda e: e.collective_compute(
                "AllGather", ALU.bypass, replica_groups=[list(range(NCORES))],
                ins=[src_.ap().opt()], outs=[dst_.ap().opt()])
        op("pool", mk(), r=["kT_s" if nm == "k" else "v_s"], w=["kT_g" if nm == "k" else "v_g", "cc_chain"], dma="ag_" + nm, inc=1)

    if stage == 1:
        dma("sp", dbg["k"][:, :], kT_g[3 * NSEQ * 256:4 * NSEQ * 256, :], ["kT_g"], ["dbgk"], "dbg")
        dma("sp", dbg["v"][:, :], v_g[3 * NSEQ * 256:4 * NSEQ * 256, :], ["v_g"], ["dbgv"], "dbg")
        TR.emit(nc)
        return nc

    wv = {k: wb_g[k][:, :].rearrange("(kc p) c -> p kc c", p=128) for k in wb_g}
    BQ = lambda h: prmP[:, P_BIN + h:P_BIN + h + 1]

    ucache = {}

    def attention_head(s, h):
        kvh = h // 4
        j = 0
        KB = TL // 128
        nblk = NCORES * KB
        pend = None
        for r_ in range(NCORES):
            ks = r_ % 2
            row0 = r_ * NSEQ * 256 + (s * 2 + kvh) * 128
            dma("sp", Kring[:, ks, 0:TL], kT_g[row0:row0 + 128, :], ["kT_g", "arenaA"], ["K%d" % ks], "K%d" % ks)
            dma("sp", Vring[:, ks, 0:TL], v_g[row0:row0 + 128, :], ["v_g", "arenaA"], ["V%d" % ks], "V%d" % ks)
            for kb in range(KB):
                sbank = 4 + (j % 2)
                mm(ps[sbank][:, :], Kring[:, ks, kb * 128:(kb + 1) * 128], qT[:, h, :], True, True,
                   ["K%d" % ks, "qT%d" % h, "arenaA"], ["ps%d" % sbank])
                if pend is not None:
                    pend()
                pslot = j % 3

                def mk(j=j, ks=ks, kb=kb, pslot=pslot, sbank=sbank):
                    def f():
                        mm(ps[6][:, :], Vring[:, ks, kb * 128:(kb + 1) * 128], PT[:, pslot, :], j == 0, j == nblk - 1,
                           ["V%d" % ks, "PT%d" % pslot, "arenaA"], ["ps6"])
                        mm(ps[7][:, :], ones_b[:, :], PT[:, pslot, :], j == 0, j == nblk - 1,
                           ["PT%d" % pslot, "ones_b"], ["ps7"])
                    return f
                act(PT[:, pslot, :], ps[sbank][:, :], AF.Exp, ["ps%d" % sbank], ["PT%d" % pslot], scale=float(SCALE))
                pend = mk()
                j += 1
        pend()
        ta = nxt("tmp", 4)
        recip(tmpf[:, ta, :], ps[7][:, :], ["ps7"], ["tmp%d" % ta])
        tt("dve", ao[:, h, :], ps[6][:, :], tmpf[:, ta, :], ALU.mult, ["ps6", "tmp%d" % ta], ["ao%d" % h])

    def u_proj(s, ti, cc):
        gs = cc % 2
        gk_ = "glu%d" % gs
        if cc % 2 == 0:
            c0 = (cc // 2) * 256
            ucache["val"] = wblock("win", wv["win"], (1536 + c0, 1536 + c0 + 256), KC, 256)
            ucache["gat"] = wblock("win", wv["win"], (2560 + c0, 2560 + c0 + 256), KC, 256)
        wval, wvk = ucache["val"]
        wgat, wgk = ucache["gat"]
        o2 = (cc % 2) * 128
        wval = wval[:, :, o2:o2 + 128]
        wgat = wgat[:, :, o2:o2 + 128]
        for kc in range(KC):
            mm(ps[0][:, :], wval[:, kc, :], hb[:, kc, :], kc == 0, kc == KC - 1, [wvk, hbk[kc]], ["ps0"])
        for kc in range(KC):
            mm(ps[1][:, :], wgat[:, kc, :], hb[:, kc, :], kc == 0, kc == KC - 1, [wgk, hbk[kc]], ["ps1"])
        for kc in range(KC):
            mm(ps[2][:, 0:32], wval[:, kc, :], hbh[:, kc, :], kc == 0, kc == KC - 1, [wvk, "hbh"], ["ps2"])
        for kc in range(KC):
            mm(ps[2][:, 32:64], wgat[:, kc, :], hbh[:, kc, :], kc == 0, kc == KC - 1, [wgk, "hbh"], ["ps2"])
        bval = prmP[:, P_BIN + 12 + cc:P_BIN + 13 + cc]
        bgat = prmP[:, P_BIN + 20 + cc:P_BIN + 21 + cc]
        ta = nxt("tmp", 4)
        tak = "tmp%d" % ta
        act(tmpf[:, ta, :], ps[1][:, :], AF.Sigmoid, ["ps1", "prmP"], [tak], bias=bgat)
        stt(glu[:, gs, 15:15 + T], ps[0][:, :], bval, tmpf[:, ta, :], ALU.add, ALU.mult, ["ps0", tak, "prmP"], [gk_])
        act(gluh[:, 32:64], ps[2][:, 32:64], AF.Sigmoid, ["ps2", "prmP"], ["gluh"], bias=bgat)
        stt(gluh[:, 0:32], ps[2][:, 0:32], bval, gluh[:, 32:64], ALU.add, ALU.mult, ["ps2", "gluh", "prmP"], ["gluh"])
        ml = hmask[:, 0:1] if ti == 0 else 1.0
        mr = hmask[:, 1:2] if ti == NT - 1 else 1.0
        ts("dve", glu[:, gs, 0:15], gluh[:, 1:16], ml, None, ALU.mult, None, ["gluh", "const"], [gk_])
        ts("dve", glu[:, gs, 15 + T:30 + T], gluh[:, 16:31], mr, None, ALU.mult, None, ["gluh", "const"], [gk_])

    def conv(cc):
        gs = cc % 2
        gk_ = "glu%d" % gs
        ck = "cacc%d" % cc
        ts("dve", cacc[:, cc, :], glu[:, gs, 0:T], wdwP[:, cc, 0:1], prmP[:, P_BDW + cc:P_BDW + cc + 1],
           ALU.mult, ALU.add, [gk_, "wdwP", "prmP", "arenaA"], [ck])
        for jt in range(1, 31):
            stt(cacc[:, cc, :], glu[:, gs, jt:jt + T], wdwP[:, cc, jt:jt + 1], cacc[:, cc, :], ALU.mult, ALU.add,
                [gk_, "wdwP", ck], [ck])

    def conv_ln_swish():
        for cc in range(8):
            mm(ps[0][:, :], ones_f[:, :], cacc[:, cc, :], cc == 0, cc == 7, ["cacc%d" % cc, "ones_f", "arenaA"], ["ps0"])
        for cc in range(8):
            ta = nxt("tmp", 4)
            act(tmpf[:, ta, :], cacc[:, cc, :], AF.Square, ["cacc%d" % cc], ["tmp%d" % ta])
            mm(ps[1][:, :], ones_f[:, :], tmpf[:, ta, :], cc == 0, cc == 7, ["tmp%d" % ta, "ones_f"], ["ps1"])
        tm = nxt("tmp", 4)
        tmk = "tmp%d" % tm
        ts("dve", tmpf[:, tm, :], ps[0][:, :], 1.0 / 1024.0, None, ALU.mult, None, ["ps0"], [tmk])
        tv = nxt("tmp", 4)
        tvk = "tmp%d" % tv
        tt("dve", tmpf[:, tv, :], tmpf[:, tm, :], tmpf[:, tm, :], ALU.mult, [tmk], [tvk])
        stt(tmpf[:, tv, :], ps[1][:, :], 1.0 / 1024.0, tmpf[:, tv, :], ALU.mult, ALU.subtract, ["ps1", tvk], [tvk])
        act(tmpf[:, tv, :], tmpf[:, tv, :], AF.Sqrt, [tvk, "eps"], [tvk], bias=epsT[:, 0:1])
        recip(tmpf[:, tv, :], tmpf[:, tv, :], [tvk], [tvk])
        for cc in range(8):
            tc = nxt("tmp", 4)
            while tc in (tm, tv):
                tc = nxt("tmp", 4)
            tck = "tmp%d" % tc
            tt("dve", tmpf[:, tc, :], cacc[:, cc, :], tmpf[:, tm, :], ALU.subtract, ["cacc%d" % cc, tmk], [tck])
            tt("dve", tmpf[:, tc, :], tmpf[:, tc, :], tmpf[:, tv, :], ALU.mult, [tck, tvk], [tck])
            act(hb[:, 8 + cc, :], tmpf[:, tc, :], AF.Silu, [tck, "prmP"], [hbk[8 + cc]],
                bias=prmP[:, P_BCN + cc:P_BCN + cc + 1], scale=prmP[:, P_GCN + cc:P_GCN + cc + 1])

    def proj_resid(k, s, G, src_of):
        for fp in range(8):
            wblk, wk = wblock(k, wv[k], (fp * 256, (fp + 1) * 256), KC, 256)
            for f2 in range(2):
                fc = fp * 2 + f2
                b = nxt("psb", 4)
                pk = "ps%d" % b
                for kc in range(KC):
                    src, skey = src_of(kc)
                    mm(ps[b][:, :], wblk[:, kc, f2 * 128:(f2 + 1) * 128], src, kc == 0, kc == KC - 1, [wk, skey], [pk])
                stt(xT[:, fc, :], ps[b][:, :], G[:, fc:fc + 1], xT[:, fc, :], ALU.mult, ALU.add,
                    [pk, xTk[fc], "modp"], [xTk[fc]])

    def ffn(s):
        gate(["arenaA", "arenaB"])
        for jp in range(FC // 2):
            wg, wgk = wblock("wgate", wv["wgate"], (jp * 256, (jp + 1) * 256), KC, 256)
            wu, wuk = wblock("wup", wv["wup"], (jp * 256, (jp + 1) * 256), KC, 256)
            for j2 in range(2):
                jc = jp * 2 + j2
                bg = nxt("psb", 4)
                for kc in range(KC):
                    mm(ps[bg][:, :], wg[:, kc, j2 * 128:(j2 + 1) * 128], hb[:, kc, :], kc == 0, kc == KC - 1,
                       [wgk, hbk[kc]], ["ps%d" % bg])
                bu = nxt("psb", 4)
                for kc in range(KC):
                    mm(ps[bu][:, :], wu[:, kc, j2 * 128:(j2 + 1) * 128], hb[:, kc, :], kc == 0, kc == KC - 1,
                       [wuk, hbk[kc]], ["ps%d" % bu])
                ta = nxt("tmp", 4)
                act(tmpf[:, ta, :], ps[bg][:, :], AF.Silu, ["ps%d" % bg], ["tmp%d" % ta])
                tt("dve", actT[:, jc, :], ps[bu][:, :], tmpf[:, ta, :], ALU.mult, ["ps%d" % bu, "tmp%d" % ta, "arenaB"],
                   ["actT%d" % jc])
        G2 = modp[:, s, 80:96]
        wdv = wb_g["wdown"][:, :].rearrange("(kc p) c -> p kc c", p=128)
        for fg in range(4):
            banks = [4, 5, 6, 7] if fg % 2 == 0 else [0, 1, 2, 3]
            for kq in range(4):
                slot = nxt("w", 3)
                wk = "w%d" % slot
                view = wring[:, slot, 0:11 * 512].rearrange("p (a b) -> p a b", b=512)
                dma("sp", view, wdv[:, kq * 11:(kq + 1) * 11, fg * 512:(fg + 1) * 512], ["wbg_wdown"], [wk], wk)
                for f4 in range(4):
                    for kk in range(11):
                        jc = kq * 11 + kk
                        mm(ps[banks[f4]][:, :], view[:, kk, f4 * 128:(f4 + 1) * 128], actT[:, jc, :],
                           kq == 0 and kk == 0, kq == 3 and kk == 10, [wk, "actT%d" % jc, "arenaB"],
                           ["ps%d" % banks[f4]])
            for f4 in range(4):
                fc = fg * 4 + f4
                stt(xT[:, fc, :], ps[banks[f4]][:, :], G2[:, fc:fc + 1], xT[:, fc, :], ALU.mult, ALU.add,
                    ["ps%d" % banks[f4], xTk[fc], "modp"], [xTk[fc]])

    def store_tile(s, ti):
        t0 = ti * T
        SKIP = os.environ.get("K_SKIP", "").split(",")
        for sub in range(4):
            slot = nxt("stg", 2)
            sk = "stg%d" % slot
            for kg in range(4):
                if "store_tp" in SKIP:
                    continue
                b = nxt("psb", 4)
                for j in range(4):
                    kc = kg * 4 + j
                    tp(ps[b][:, j * 128:(j + 1) * 128], xT[:, kc, sub * 128:(sub + 1) * 128], ident[:, :],
                       [xTk[kc], "const"], ["ps%d" % b])
                cp(evac_eng(), stg[:, slot, kg * 512:(kg + 1) * 512], ps[b][:, :], ["ps%d" % b], [sk])
            if "store_dma" not in SKIP:
                dma("sp", y[s, t0 + sub * 128:t0 + (sub + 1) * 128, :], stg[:, slot, :], [sk], ["y"], sk)

    MAXT = int(os.environ.get("K_MAXT", "99"))
    STEP = int(os.environ.get("K_STEP", "99"))
    ntile = 0
    for s in range(NSEQ):
        for ti in range(NT):
            if ntile >= MAXT:
                break
            ntile += 1
            SKIP = os.environ.get("K_SKIP", "").split(",")
            if "gate" not in SKIP:
                gate(["arenaA", "arenaB"])
            load_cs(ti)
            load_x_tile(s, ti)
            if "halo" not in SKIP:
                load_halo(s, ti)
            norm(xT, xTk, hb, hbk, A1[:, s, :], modp[:, s, 0:16], T, float(D))
            if "hnorm" not in SKIP:
                norm(xTh, ["xTh"] * KC, hbh, ["hbh"] * KC, A1[:, s, :], modp[:, s, 0:16], 32, float(D))
            if STEP < 1:
                if "store" not in SKIP:
                    store_tile(s, ti)
                continue
            for hp in range(4):
                wq, wqk = wblock("win", wv["win"], (hp * 256, (hp + 1) * 256), KC, 256)
                for hh in range(2):
                    h = hp * 2 + hh
                    b = nxt("psb", 4)
                    pk = "ps%d" % b
                    for kc in range(KC):
                        mm(ps[b][:, :], wq[:, kc, hh * 128:(hh + 1) * 128], hb[:, kc, :], kc == 0, kc == KC - 1,
                           [wqk, hbk[kc]], [pk])
                    qk_post(ps[b][:, :], pk, BQ(h), prmP[:, P_GQ:P_GQ + 1], qT[:, h, :], "qT%d" % h)
            if STEP < 2:
                if "store" not in SKIP:
                    store_tile(s, ti)
                continue
            for h in range(8):
                u_proj(s, ti, h)
                if STEP >= 3:
                    attention_head(s, h)
                conv(h)
            conv_ln_swish()
            if STEP < 4:
                store_tile(s, ti)
                continue
            G1 = modp[:, s, 32:48]
            proj_resid("wout", s, G1,
                       lambda kc: (ao[:, kc, :], "ao%d" % kc) if kc < 8 else (hb[:, kc, :], hbk[kc]))
            if STEP < 5:
                store_tile(s, ti)
                continue
            norm(xT, xTk, hb, hbk, A2[:, s, :], modp[:, s, 48:64], T, float(D))
            ffn(s)
            norm(xT, xTk, xT, xTk, prmP[:, P_GF:P_GF + 16], None, T, float(D))
            store_tile(s, ti)

    TR.emit(nc)
    return nc


def _rope_tables(core):
    pos = core * TL + np.arange(TL)
    row = (pos // 64).astype(np.float32)
    col = (pos % 64).astype(np.float32)
    inv = (np.float32(10000.0) ** (-(np.arange(32, dtype=np.float32) * np.float32(2.0) / np.float32(64.0)))).astype(np.float32)
    ang_r = (row[:, None] * inv[None, :]).astype(np.float32)
    ang_c = (col[:, None] * inv[None, :]).astype(np.float32)
    cos_t = np.empty((128, TL), np.float32)
    sin_t = np.empty((128, TL), np.float32)
    cos_t[0:32] = np.cos(ang_r).T
    cos_t[32:64] = np.cos(ang_r).T
    cos_t[64:96] = np.cos(ang_c).T
    cos_t[96:128] = np.cos(ang_c).T
    sin_t[0:32] = np.sin(ang_r).T
    sin_t[32:64] = np.sin(ang_r).T
    sin_t[64:96] = np.sin(ang_c).T
    sin_t[96:128] = np.sin(ang_c).T
    return cos_t, sin_t


def _rot_lhsT():
    m = np.zeros((128, 128), np.float32)
    for base in (0, 64):
        for i in range(32):
            m[base + i + 32, base + i] = -1.0
            m[base + i, base + i + 32] = 1.0
    return m


def make_in_maps(inp):
    f = lambda a: np.ascontiguousarray(np.asarray(a, dtype=np.float32))
    x_all = np.concatenate([f(inp["x_prompt"]), f(inp["x_sample"])], axis=0)
    c_all = np.concatenate([f(inp["c_prompt"]), f(inp["c_sample"])], axis=0)
    c48 = np.ascontiguousarray(c_all.reshape(3, 16, 128).transpose(1, 0, 2).reshape(48, 128))
    w_ada = f(inp["w_ada"])[0]
    b_ada = f(inp["b_ada"])[0]
    b_in = f(inp["b_in"])[0]
    prm = np.concatenate([
        f(inp["g_norm1"])[0].reshape(16, 128), f(inp["g_norm2"])[0].reshape(16, 128),
        f(inp["g_final"]).reshape(16, 128), b_in.reshape(28, 128),
        f(inp["g_q"])[0].reshape(1, 128), f(inp["g_k"])[0].reshape(1, 128),
        f(inp["b_dw"])[0].reshape(8, 128), f(inp["g_cn"])[0].reshape(8, 128), f(inp["b_cn"])[0].reshape(8, 128)],
        axis=0)
    assert prm.shape == (P_ROWS, 128)
    w_in = f(inp["w_in"])[0]
    w_out = f(inp["w_out"])[0]
    w_gate = f(inp["w_gate"])[0]
    w_up = f(inp["w_up"])[0]
    w_down = f(inp["w_down"])[0]
    w_dw = f(inp["w_dw"])[0]
    ident = np.eye(128, dtype=np.float32)
    rotT = _rot_lhsT()
    maps = []
    for c in range(NCORES):
        lo, hi = c * TL, (c + 1) * TL
        xs = np.ascontiguousarray(x_all[:, lo:hi, :])
        xh = np.zeros((NSEQ, NT, 32, D), np.float32)
        for ti in range(NT):
            t0 = lo + ti * T
            if t0 - 16 >= 0:
                xh[:, ti, 0:16] = x_all[:, t0 - 16:t0]
            if t0 + T + 16 <= SEQ:
                xh[:, ti, 16:32] = x_all[:, t0 + T:t0 + T + 16]
        cos_t, sin_t = _rope_tables(c)
        hm = np.ones((128, 2), np.float32)
        if c == 0:
            hm[:, 0] = 0.0
        if c == NCORES - 1:
            hm[:, 1] = 0.0
        maps.append(dict(
            xs=xs, xh=xh, c48=c48,
            w_ada=np.ascontiguousarray(w_ada[:, c * 1536:(c + 1) * 1536]),
            b_ada=np.ascontiguousarray(b_ada[c * 1536:(c + 1) * 1536].reshape(1, 1536)),
            prm=prm, bvrow=np.ascontiguousarray(b_in[1280:1536].reshape(1, 256)), w_dw=w_dw,
            w_in=np.ascontiguousarray(w_in[c * 256:(c + 1) * 256]),
            w_out=np.ascontiguousarray(w_out[c * 256:(c + 1) * 256]),
            w_gate=np.ascontiguousarray(w_gate[c * 256:(c + 1) * 256]),
            w_up=np.ascontiguousarray(w_up[c * 256:(c + 1) * 256]),
            w_down=np.ascontiguousarray(w_down[c * 704:(c + 1) * 704]),
            ident=ident, rotT=rotT, cos_t=cos_t, sin_t=sin_t, hmask=hm))
    return maps


_NC_CACHE = {}


def kernel(**inputs):
    maps = make_in_maps(inputs)
    nc = build(2)
    res = run_bass_kernel_spmd(nc, maps, core_ids=list(range(NCORES)))
    ys = [np.asarray(r["y"], dtype=np.float32) for r in res.results]
    full = np.concatenate(ys, axis=1)
    y_prompt = np.ascontiguousarray(full[0:1])
    y_sample = np.ascontiguousarray(full[1:3])
    return (y_prompt, y_sample)
```
